# Optimizing a Trainium2 kernel written in Bass

```python
import math
import jax, jax.numpy as jnp
from jax import lax
import numpy as np

D_MODEL = 2048
BATCH = 2
SEQ = 4096
DEPTH = 2

EPS = 1e-6
GLA_HEADS = 4
GLA_DK = D_MODEL // 16
GLA_DV = D_MODEL // 8
GLA_LOWRANK = 16
GLA_TAU = 16.0
GLA_CHUNK = 64
FOX_HEADS = 8
FOX_DH = D_MODEL // 16
FOX_W = FOX_HEADS * FOX_DH
FOX_BLOCK = 128
S5_WIDTH = D_MODEL
S5_GROUP_CH = 16
S5_GROUPS = S5_WIDTH // S5_GROUP_CH
S5_STATE = 64
N_EVEN = (DEPTH + 1) // 2
N_ODD = DEPTH // 2
AB_SIZES = (GLA_HEADS * GLA_DK, GLA_HEADS * GLA_DK, GLA_HEADS * GLA_DV, GLA_LOWRANK, GLA_HEADS * GLA_DV,
            FOX_W, FOX_W, FOX_W, FOX_HEADS, FOX_W)
AB_IN = sum(AB_SIZES)
MIX_W = GLA_HEADS * GLA_DV + FOX_W

kernel_name = "hybrid_gla_fox_s5_trunk"


def rms_norm(x, g):
    xf = x.astype(jnp.float32)
    y = xf * lax.rsqrt(jnp.mean(xf * xf, axis=-1, keepdims=True) + EPS)
    return (y * g.astype(jnp.float32)).astype(x.dtype)


def gla_chunked(q, k, v, log_a):
    Bsz, H, L, dk = q.shape
    dv = v.shape[-1]
    n = L // GLA_CHUNK

    def to_chunks(t):
        t = t.astype(jnp.float32).reshape(Bsz, H, n, GLA_CHUNK, t.shape[-1])
        return jnp.moveaxis(t, 2, 0)

    qc, kc, vc, gc = (to_chunks(q * (dk ** -0.5)), to_chunks(k), to_chunks(v), to_chunks(log_a))
    mask = jnp.tril(jnp.ones((GLA_CHUNK, GLA_CHUNK), dtype=bool))[:, :, None]

    def step(S, inp):
        qi, ki, vi, gi = inp
        b = jnp.cumsum(gi, axis=-2)
        o_inter = jnp.einsum('bhck,bhkv->bhcv', qi * jnp.exp(b), S)
        rel = b[..., :, None, :] - b[..., None, :, :]
        decay = jnp.exp(jnp.where(mask, rel, -jnp.inf))
        attn = jnp.einsum('bhik,bhjk,bhijk->bhij', qi, ki, decay)
        o_intra = jnp.einsum('bhij,bhjv->bhiv', attn, vi)
        b_last = b[..., -1:, :]
        S = jnp.exp(b_last[..., 0, :])[..., None] * S + jnp.einsum(
            'bhck,bhcv->bhkv', ki * jnp.exp(b_last - b), vi)
        return S, o_inter + o_intra

    S0 = jnp.zeros((Bsz, H, dk, dv), jnp.float32)
    _, out = lax.scan(step, S0, (qc, kc, vc, gc))
    return jnp.moveaxis(out, 0, 2).reshape(Bsz, H, L, dv)


def forgetting_attention(q, k, v, log_f):
    L, d = q.shape[2], q.shape[3]
    scale = d ** -0.5
    c = jnp.cumsum(log_f, axis=-1)
    outs = []
    for blk in range(L // FOX_BLOCK):
        s0, s1 = blk * FOX_BLOCK, (blk + 1) * FOX_BLOCK
        qb, kb, vb = q[:, :, s0:s1], k[:, :, :s1], v[:, :, :s1]
        logits = (jnp.einsum('bhqd,bhkd->bhqk', qb, kb).astype(jnp.float32) * scale
                  + c[:, :, s0:s1, None] - c[:, :, None, :s1])
        causal = jnp.arange(s0, s1)[:, None] >= jnp.arange(s1)[None, :]
        p = jax.nn.softmax(jnp.where(causal, logits, -jnp.inf), axis=-1)
        outs.append(jnp.einsum('bhqk,bhkd->bhqd', p.astype(v.dtype), vb))
    return jnp.concatenate(outs, axis=2)


def _complex_combine(e1, e2):
    a1r, a1i, b1r, b1i = e1
    a2r, a2i, b2r, b2i = e2
    ar = a2r * a1r - a2i * a1i
    ai = a2r * a1i + a2i * a1r
    br = a2r * b1r - a2i * b1i + b2r
    bi = a2r * b1i + a2i * b1r + b2i
    return ar, ai, br, bi


def s5_ssm(u, lam_re, lam_im, log_step, b_re, b_im, c_re, c_im, d):
    Bsz, L, E = u.shape
    f32 = jnp.float32
    uf = u.astype(f32).reshape(Bsz, L, S5_GROUPS, S5_GROUP_CH)
    lr = jnp.minimum(lam_re.astype(f32), -1e-4)
    li = lam_im.astype(f32)
    step = jnp.exp(log_step.astype(f32))[:, None]
    mag = jnp.exp(lr * step)
    lb_re, lb_im = mag * jnp.cos(li * step), mag * jnp.sin(li * step)
    den = lr * lr + li * li
    nr, ni = lb_re - 1.0, lb_im
    coef_re = (nr * lr + ni * li) / den
    coef_im = (ni * lr - nr * li) / den
    br_, bi_ = b_re.astype(f32), b_im.astype(f32)
    bb_re = coef_re[..., None] * br_ - coef_im[..., None] * bi_
    bb_im = coef_re[..., None] * bi_ + coef_im[..., None] * br_
    bu_re = jnp.einsum('gpc,blgc->blgp', bb_re, uf)
    bu_im = jnp.einsum('gpc,blgc->blgp', bb_im, uf)
    a_re = jnp.broadcast_to(lb_re, (1, L, S5_GROUPS, S5_STATE))
    a_im = jnp.broadcast_to(lb_im, (1, L, S5_GROUPS, S5_STATE))
    _, _, x_re, x_im = lax.associative_scan(_complex_combine, (a_re, a_im, bu_re, bu_im), axis=1)
    y = (jnp.einsum('gcp,blgp->blgc', c_re.astype(f32), x_re)
         - jnp.einsum('gcp,blgp->blgc', c_im.astype(f32), x_im))
    return y.reshape(Bsz, L, E) + d.astype(f32) * uf.reshape(Bsz, L, E)


def layer_ab(x, norm_g, w_in, alpha_up, alpha_b, head_g, f_b, w_out):
    Bsz, L, _ = x.shape
    h = rms_norm(x, norm_g)
    split_idx = np.cumsum(AB_SIZES)[:-1].tolist()
    g_q, g_k, g_v, g_lr, g_gate, f_q, f_k, f_v, f_f, f_gate = jnp.split(h @ w_in, split_idx, axis=-1)

    def heads(t, nh):
        return t.reshape(Bsz, L, nh, -1).transpose(0, 2, 1, 3)

    log_a = jax.nn.log_sigmoid((g_lr @ alpha_up + alpha_b).astype(jnp.float32)) / GLA_TAU
    o = gla_chunked(heads(g_q, GLA_HEADS), heads(g_k, GLA_HEADS), heads(g_v, GLA_HEADS),
                    heads(log_a, GLA_HEADS))
    o = o * lax.rsqrt(jnp.mean(o * o, axis=-1, keepdims=True) + EPS)
    o = o * head_g.astype(jnp.float32).reshape(GLA_HEADS, 1, GLA_DV)
    o_gla = o.transpose(0, 2, 1, 3).reshape(Bsz, L, -1).astype(x.dtype) * jax.nn.silu(g_gate)

    log_f = jax.nn.log_sigmoid((f_f + f_b).astype(jnp.float32)).transpose(0, 2, 1)
    o_fox = forgetting_attention(heads(f_q, FOX_HEADS), heads(f_k, FOX_HEADS), heads(f_v, FOX_HEADS), log_f)
    o_fox = o_fox.transpose(0, 2, 1, 3).reshape(Bsz, L, -1) * jax.nn.silu(f_gate)

    return x + jnp.concatenate([o_gla, o_fox], axis=-1) @ w_out


def layer_c(x, norm_g, w_in, lam_re, lam_im, log_step, b_re, b_im, c_re, c_im, d, glu_w, glu_b, w_out):
    h = rms_norm(x, norm_g)
    u, gate = jnp.split(h @ w_in, 2, axis=-1)
    y = s5_ssm(u, lam_re, lam_im, log_step, b_re, b_im, c_re, c_im, d).astype(x.dtype)
    z = jax.nn.gelu(y)
    z = z * jax.nn.sigmoid(z @ glu_w + glu_b)
    return x + (z * jax.nn.silu(gate)) @ w_out


def setup_inputs(seed: int = 0) -> dict:
    key = jax.random.key(seed)
    ks = jax.random.split(key, 24)
    nrm = jax.random.normal
    f32 = jnp.float32
    D, E, G, P, Cg = D_MODEL, S5_WIDTH, S5_GROUPS, S5_STATE, S5_GROUP_CH
    x = nrm(ks[0], (BATCH, SEQ, D), f32)
    ab_norm_g = 1.0 + 0.01 * nrm(ks[1], (N_EVEN, D), f32)
    ab_w_in = nrm(ks[2], (N_EVEN, D, AB_IN), f32) * D ** -0.5
    gla_alpha_up = nrm(ks[3], (N_EVEN, GLA_LOWRANK, GLA_HEADS * GLA_DK), f32) * GLA_LOWRANK ** -0.5
    gla_alpha_b = 0.01 * nrm(ks[4], (N_EVEN, GLA_HEADS * GLA_DK), f32)
    gla_head_g = 1.0 + 0.01 * nrm(ks[5], (N_EVEN, GLA_HEADS * GLA_DV), f32)
    fox_f_b = 1.0 + 0.1 * nrm(ks[6], (N_EVEN, FOX_HEADS), f32)
    ab_w_out = nrm(ks[7], (N_EVEN, MIX_W, D), f32) * MIX_W ** -0.5
    c_norm_g = 1.0 + 0.01 * nrm(ks[8], (N_ODD, D), f32)
    c_w_in = nrm(ks[9], (N_ODD, D, 2 * E), f32) * D ** -0.5
    n_idx = jnp.arange(P, dtype=f32)
    s5_lambda_re = -0.5 + 0.01 * nrm(ks[10], (N_ODD, G, P), f32)
    s5_lambda_im = math.pi * n_idx + 0.01 * nrm(ks[11], (N_ODD, G, P), f32)
    s5_log_step = jax.random.uniform(ks[12], (N_ODD, G), f32, math.log(1e-3), math.log(1e-1))
    s5_b_re = nrm(ks[13], (N_ODD, G, P, Cg), f32) * (2.0 * Cg) ** -0.5
    s5_b_im = nrm(ks[14], (N_ODD, G, P, Cg), f32) * (2.0 * Cg) ** -0.5
    s5_c_re = nrm(ks[15], (N_ODD, G, Cg, P), f32) * (2.0 * P) ** -0.5
    s5_c_im = nrm(ks[16], (N_ODD, G, Cg, P), f32) * (2.0 * P) ** -0.5
    s5_d = nrm(ks[17], (N_ODD, E), f32)
    glu_w = nrm(ks[18], (N_ODD, E, E), f32) * E ** -0.5
    glu_b = 0.01 * nrm(ks[19], (N_ODD, E), f32)
    c_w_out = nrm(ks[20], (N_ODD, E, D), f32) * E ** -0.5
    final_norm_g = 1.0 + 0.01 * nrm(ks[21], (D,), f32)
    return {"x": x, "ab_norm_g": ab_norm_g, "ab_w_in": ab_w_in, "gla_alpha_up": gla_alpha_up,
            "gla_alpha_b": gla_alpha_b, "gla_head_g": gla_head_g, "fox_f_b": fox_f_b,
            "ab_w_out": ab_w_out, "c_norm_g": c_norm_g, "c_w_in": c_w_in,
            "s5_lambda_re": s5_lambda_re, "s5_lambda_im": s5_lambda_im, "s5_log_step": s5_log_step,
            "s5_b_re": s5_b_re, "s5_b_im": s5_b_im, "s5_c_re": s5_c_re, "s5_c_im": s5_c_im,
            "s5_d": s5_d, "glu_w": glu_w, "glu_b": glu_b, "c_w_out": c_w_out,
            "final_norm_g": final_norm_g}


def reference(x, ab_norm_g, ab_w_in, gla_alpha_up, gla_alpha_b, gla_head_g, fox_f_b, ab_w_out,
              c_norm_g, c_w_in, s5_lambda_re, s5_lambda_im, s5_log_step, s5_b_re, s5_b_im,
              s5_c_re, s5_c_im, s5_d, glu_w, glu_b, c_w_out, final_norm_g):
    for layer in range(DEPTH):
        i = layer // 2
        if layer % 2 == 0:
            x = layer_ab(x, ab_norm_g[i], ab_w_in[i], gla_alpha_up[i], gla_alpha_b[i],
                         gla_head_g[i], fox_f_b[i], ab_w_out[i])
        else:
            x = layer_c(x, c_norm_g[i], c_w_in[i], s5_lambda_re[i], s5_lambda_im[i], s5_log_step[i],
                        s5_b_re[i], s5_b_im[i], s5_c_re[i], s5_c_im[i], s5_d[i], glu_w[i], glu_b[i],
                        c_w_out[i])
    return rms_norm(x, final_norm_g)
```

```python
import numpy as np
import ml_dtypes
from contextlib import ExitStack
import concourse.bass as bass
import concourse.mybir as mybir
from concourse.bass_utils import run_bass_kernel_spmd

F32 = mybir.dt.float32
BF16 = mybir.dt.bfloat16
AF = mybir.ActivationFunctionType
ALU = mybir.AluOpType
AX = mybir.AxisListType
NPBF = ml_dtypes.bfloat16

ENGS = ("pe", "dve", "act", "pool", "sp")
NCORES = 8
D = 2048
EPS = 1e-6


class KB:
    def __init__(self, nc, stack, n_dma_sems=32):
        self.nc = nc
        self.stack = stack
        self.q = {e: [] for e in ENGS}
        self.cnt = {e: 0 for e in ENGS}
        self.sem = {e: stack.enter_context(nc.semaphore("s_" + e)) for e in ENGS}
        self.seen = {}
        self.lastw = {}
        self.readers = {}
        self.dma_pool = [[stack.enter_context(nc.semaphore("d%d" % i)), 0, None] for i in range(n_dma_sems)]
        self.dma_rr = 0
        self.ntens = 0

    def sb(self, shape, dt=F32, name=None):
        self.ntens += 1
        return self.stack.enter_context(self.nc.sbuf_tensor("sb_" + (name or ("t%d" % self.ntens)), list(shape), dt))

    def ps(self, shape, dt=F32, name=None):
        self.ntens += 1
        return self.stack.enter_context(self.nc.psum_tensor("ps_" + (name or ("p%d" % self.ntens)), list(shape), dt))

    def _wait(self, eng, tok):
        if tok is None:
            return
        if tok[0] == "dma":
            _, sem, val, idx = tok
            key = (eng, "dma", idx)
        else:
            p, val = tok
            if p == "pe" and eng == "pe":
                return
            key = (eng, p)
            sem = self.sem[p]
        if self.seen.get(key, 0) >= val:
            return
        self.seen[key] = val
        self.q[eng].append(lambda e, sem=sem, val=val: e.wait_ge(sem, val))

    def _deps(self, eng, reads, writes):
        for k in reads:
            self._wait(eng, self.lastw.get(k))
        for k in writes:
            self._wait(eng, self.lastw.get(k))
            for t in self.readers.get(k, {}).values():
                self._wait(eng, t)

    def _commit(self, tok, reads, writes):
        rk = tok[3] if tok[0] == "dma" else tok[0]
        rk = ("dma", rk) if tok[0] == "dma" else rk
        for k in reads:
            self.readers.setdefault(k, {})[rk] = tok
        for k in writes:
            self.lastw[k] = tok
            self.readers[k] = {}

    def op(self, eng, fn, reads=(), writes=(), inc=True):
        self._deps(eng, reads, writes)
        if inc:
            self.cnt[eng] += 1
            sem = self.sem[eng]
            self.q[eng].append(lambda e, fn=fn, sem=sem: fn(e).then_inc(sem, 1))
            tok = (eng, self.cnt[eng])
        else:
            self.q[eng].append(lambda e, fn=fn: fn(e))
            tok = (eng, self.cnt[eng] + 1)
        self._commit(tok, reads, writes)
        return tok

    def dma(self, eng, out, in_, reads=(), writes=(), **kw):
        self._deps(eng, reads, writes)
        slot = self.dma_pool[self.dma_rr]
        idx = self.dma_rr
        self.dma_rr = (self.dma_rr + 1) % len(self.dma_pool)
        if slot[2] is not None:
            self._wait(eng, slot[2])
        slot[1] += 16
        sem, val = slot[0], slot[1]
        tok = ("dma", sem, val, idx)
        slot[2] = tok
        self.q[eng].append(
            lambda e, out=out, in_=in_, sem=sem, kw=kw: e.dma_start(out=out, in_=in_, **kw).then_inc(sem, 16))
        self._commit(tok, reads, writes)
        return tok

    def finish(self, eng, keys):
        for kk in keys:
            self._wait(eng, self.lastw.get(kk))

    def emit(self):
        with self.nc.Block() as block:
            @block.tensor
            def _(e):
                for f in self.q["pe"]:
                    f(e)

            @block.vector
            def _(e):
                for f in self.q["dve"]:
                    f(e)

            @block.scalar
            def _(e):
                for f in self.q["act"]:
                    f(e)

            @block.gpsimd
            def _(e):
                for f in self.q["pool"]:
                    f(e)

            @block.sync
            def _(e):
                for f in self.q["sp"]:
                    f(e)


class Rot:
    def __init__(self, k, n, shape, dt=F32, psum=False, name="r"):
        self.tiles = [(k.ps(shape, dt, "%s%d" % (name, i)) if psum else k.sb(shape, dt, "%s%d" % (name, i)))
                      for i in range(n)]
        self.keys = ["%s%d" % (name, i) for i in range(n)]
        self.i = 0

    def next(self):
        t, key = self.tiles[self.i], self.keys[self.i]
        self.i = (self.i + 1) % len(self.tiles)
        return t, key


def make_ident(k, dt=F32):
    ident = k.sb([128, 128], dt, "ident")
    k.op("pool", lambda e: e.memset(ident[:], 1.0), writes=["ident"])
    k.op("pool", lambda e: e.affine_select(out=ident[:], in_=ident[:], pattern=[[-1, 128]],
                                           compare_op=ALU.is_equal, fill=0.0, base=0, channel_multiplier=1),
         reads=["ident"], writes=["ident"])
    return ident


def rms_scale(k, xt, xkey, scr, ss, rstd, tag):
    k.op("act", lambda e: e.activation(out=scr[:], in_=xt[:], func=AF.Square, accum_out=ss[:]),
         reads=[xkey], writes=["scr" + tag, "ss" + tag])
    k.op("dve", lambda e: e.tensor_scalar(out=rstd[:], in0=ss[:], scalar1=1.0 / D, scalar2=EPS,
                                          op0=ALU.mult, op1=ALU.add),
         reads=["ss" + tag], writes=["rstd" + tag])
    k.op("act", lambda e: e.activation(out=rstd[:], in_=rstd[:], func=AF.Sqrt),
         reads=["rstd" + tag], writes=["rstd" + tag])
    k.op("dve", lambda e: e.reciprocal(out=rstd[:], in_=rstd[:]), reads=["rstd" + tag], writes=["rstd" + tag])


def tok_to_feat(k, src, skey, hT, hkey, tt, ident, pst, gT=None, evac="dve"):
    for q4 in range(4):
        pt, pk = pst.next()
        for j in range(4):
            kc = q4 * 4 + j
            k.op("pe", lambda e, pt=pt, j=j, kc=kc: e.transpose(out=pt[:, j * 128:(j + 1) * 128],
                                                               in_=src[:, kc * 128:(kc + 1) * 128],
                                                               identity=ident[:]),
                 reads=[skey, "ident"], writes=[pk], inc=(j == 3))
        o = hT[:, q4 * 4:q4 * 4 + 4, tt * 128:(tt + 1) * 128]
        i0 = pt[:].rearrange("p (a b) -> p a b", a=4)
        if gT is not None:
            i1 = gT[:, q4 * 4:q4 * 4 + 4].unsqueeze(2).to_broadcast([128, 4, 128])
            k.op("dve", lambda e, o=o, i0=i0, i1=i1: e.tensor_tensor(out=o, in0=i0, in1=i1, op=ALU.mult),
                 reads=[pk, "gT"], writes=[hkey])
        elif evac == "dve":
            k.op("dve", lambda e, o=o, i0=i0: e.tensor_copy(out=o, in_=i0), reads=[pk], writes=[hkey])
        else:
            k.op("act", lambda e, o=o, i0=i0: e.activation(out=o, in_=i0, func=AF.Copy), reads=[pk], writes=[hkey])


def proj_groups(k, hT, hkeys, w_d, groups, NT, wrot, psr, stg, evac_state, post=None):
    wv = w_d.rearrange("(kc k) c -> k kc c", k=128)
    for (c0, cw, outs) in groups:
        Wt, wk = wrot.next()
        for qq in range(4):
            k.dma("pool", Wt[:, qq * 4:(qq + 1) * 4, 0:cw], wv[:, qq * 4:(qq + 1) * 4, c0:c0 + cw],
                  writes=[wk])
        for (layout, out_fn, odt) in outs:
            if layout == "T":
                for m in range((cw + 127) // 128):
                    mw = min(128, cw - m * 128)
                    for th in range(NT // 512):
                        pt, pk = psr.next()
                        for kc in range(16):
                            k.op("pe", lambda e, pt=pt, kc=kc, m=m, mw=mw, th=th, Wt=Wt: e.matmul(
                                pt[0:mw, :], Wt[:, kc, m * 128:m * 128 + mw], hT[:, kc, th * 512:(th + 1) * 512],
                                start=(kc == 0), stop=(kc == 15)),
                                 reads=[wk] + hkeys, writes=[pk], inc=(kc == 15))
                        st, sk = stg[odt].next()
                        _evac(k, evac_state, st[0:mw, :], pt[0:mw, :], pk, sk)
                        k.dma("sp", out_fn(m * 128, mw, th * 512, 512), st[0:mw, :], reads=[sk])
            else:
                for tt in range(NT // 128):
                    pt, pk = psr.next()
                    for kc in range(16):
                        k.op("pe", lambda e, pt=pt, kc=kc, tt=tt, Wt=Wt, cw=cw: e.matmul(
                            pt[:, 0:cw], hT[:, kc, tt * 128:(tt + 1) * 128], Wt[:, kc, 0:cw],
                            start=(kc == 0), stop=(kc == 15)),
                             reads=[wk] + hkeys, writes=[pk], inc=(kc == 15))
                    st, sk = stg[odt].next()
                    _evac(k, evac_state, st[:, 0:cw], pt[:, 0:cw], pk, sk)
                    k.dma("sp", out_fn(tt * 128, 128, 0, cw), st[:, 0:cw], reads=[sk])


def _evac(k, state, o, i, pk, sk):
    state[0] ^= 1
    if state[0]:
        k.op("dve", lambda e: e.tensor_copy(out=o, in_=i), reads=[pk], writes=[sk])
    else:
        k.op("act", lambda e: e.activation(out=o, in_=i, func=AF.Copy), reads=[pk], writes=[sk])


NT1 = 1024


def build_k1():
    nc = bass.Bass("TRN2", target_bir_lowering=False)
    x_d = nc.dram_tensor("x", [NT1, D], F32, kind="ExternalInput").ap()
    g_d = nc.dram_tensor("g", [D], F32, kind="ExternalInput").ap()
    w_d = nc.dram_tensor("w", [D, 7192], F32, kind="ExternalInput").ap()
    FA = nc.dram_tensor("FA", [1040, NT1], F32, kind="ExternalOutput").ap()
    FB = nc.dram_tensor("FB", [2048, NT1], BF16, kind="ExternalOutput").ap()
    NA = nc.dram_tensor("NA", [NT1, 2568], F32, kind="ExternalOutput").ap()
    NB = nc.dram_tensor("NB", [NT1, 2048], BF16, kind="ExternalOutput").ap()
    with ExitStack() as st:
        k = KB(nc, st)
        ident = make_ident(k)
        hT = k.sb([128, 16, NT1], BF16, "hT")
        gT = k.sb([128, 16], F32, "gT")
        k.dma("sp", gT[:], g_d.rearrange("(kc k) -> k kc", k=128), writes=["gT"], allow_slow_non_contiguous=True)
        xr = Rot(k, 2, [128, D], F32, name="xt")
        scr = k.sb([128, D], F32, "scr")
        ss = k.sb([128, 1], F32, "ss")
        rstd = k.sb([128, 1], F32, "rstd")
        pst = Rot(k, 2, [128, 512], F32, psum=True, name="pst")
        psr = Rot(k, 4, [128, 512], F32, psum=True, name="psr")
        wrot = Rot(k, 2, [128, 16, 512], BF16, name="Wt")
        stg = {F32: Rot(k, 3, [128, 512], F32, name="stf"), BF16: Rot(k, 3, [128, 512], BF16, name="stb")}
        for tt in range(NT1 // 128):
            xt, xk = xr.next()
            k.dma("sp", xt[:], x_d[tt * 128:(tt + 1) * 128, :], writes=[xk])
            rms_scale(k, xt, xk, scr, ss, rstd, "")
            k.op("dve", lambda e, xt=xt: e.tensor_scalar(out=xt[:], in0=xt[:], scalar1=rstd[:, 0:1], scalar2=None,
                                                         op0=ALU.mult),
                 reads=[xk, "rstd"], writes=[xk])
            tok_to_feat(k, xt, xk, hT, "hT", tt, ident, pst, gT=gT)

        def oT(t, off):
            return lambda r0, nr, c0, ncs: t[off + r0:off + r0 + nr, c0:c0 + ncs]

        def oN(t, off):
            return lambda r0, nr, c0, ncs: t[r0:r0 + nr, off + c0:off + c0 + ncs]

        groups = [
            (0, 512, [("T", oT(FA, 0), F32)]),
            (512, 512, [("T", oT(FA, 512), F32), ("N", oN(NA, 0), F32)]),
            (2048, 16, [("T", oT(FA, 1024), F32)]),
            (3088, 512, [("T", oT(FB, 0), BF16)]),
            (3600, 512, [("T", oT(FB, 512), BF16)]),
            (4112, 512, [("T", oT(FB, 1024), BF16)]),
            (4624, 512, [("T", oT(FB, 1536), BF16)]),
            (2064, 512, [("N", oN(NA, 512), F32)]),
            (2576, 512, [("N", oN(NA, 1024), F32)]),
            (6168, 512, [("N", oN(NA, 1536), F32)]),
            (6680, 512, [("N", oN(NA, 2048), F32)]),
            (6160, 8, [("N", oN(NA, 2560), F32)]),
            (1024, 512, [("N", oN(NB, 0), BF16)]),
            (1536, 512, [("N", oN(NB, 512), BF16)]),
            (5136, 512, [("N", oN(NB, 1024), BF16)]),
            (5648, 512, [("N", oN(NB, 1536), BF16)]),
        ]
        proj_groups(k, hT, ["hT"], w_d, groups, NT1, wrot, psr, stg, [0])
        for slot in k.dma_pool:
            if slot[2] is not None:
                k._wait("sp", slot[2])
        k.emit()
    return nc


L = 4096
NBLK = L // 128


def tri_consts(k):
    triu = k.sb([128, 128], F32, "triu")
    k.op("pool", lambda e: e.memset(triu[:], 1.0), writes=["triu"])
    k.op("pool", lambda e: e.affine_select(out=triu[:], in_=triu[:], pattern=[[1, 128]], compare_op=ALU.is_ge,
                                           fill=0.0, base=0, channel_multiplier=-1),
         reads=["triu"], writes=["triu"])
    return triu


def build_k2():
    nc = bass.Bass("TRN2", target_bir_lowering=False)
    dt_in = lambda n, s, d: nc.dram_tensor(n, s, d, kind="ExternalInput").ap()
    gqT_d = dt_in("gqT", [128, L], F32)
    gkT_d = dt_in("gkT", [128, L], F32)
    glrT_d = dt_in("glrT", [16, L], F32)
    gkN_d = dt_in("gkN", [L, 128], F32)
    ggate_d = dt_in("ggate", [L, 256], F32)
    fgate_d = dt_in("fgate", [L, 256], F32)
    ff_d = dt_in("ff", [2, 128, NBLK], F32)
    gv_d = dt_in("gv", [L, 256], BF16)
    fv_d = dt_in("fv", [L, 256], BF16)
    fqT_d = dt_in("fqT", [256, L], BF16)
    fkT_d = dt_in("fkT", [256, L], BF16)
    aup_d = dt_in("aup", [16, 128], F32)
    ab_d = dt_in("ab", [128], F32)
    hg_d = dt_in("hg", [256], F32)
    fb_d = dt_in("fb", [2], F32)
    mix_d = nc.dram_tensor("mix", [L, 512], F32, kind="ExternalOutput").ap()
    with ExitStack() as st:
        k = KB(nc, st)
        triu = tri_consts(k)
        triu16 = k.sb([128, 128], BF16, "triu16")
        k.op("dve", lambda e: e.tensor_copy(out=triu16[:], in_=triu[:]), reads=["triu"], writes=["triu16"])
        triUg = k.sb([128, 128], F32, "triUg")
        k.op("dve", lambda e: e.tensor_scalar(out=triUg[:], in0=triu[:], scalar1=-1.0 / 16, scalar2=None, op0=ALU.mult),
             reads=["triu"], writes=["triUg"])
        triSg = k.sb([128, 128], F32, "triSg")
        k.op("dve", lambda e: e.tensor_scalar(out=triSg[:], in0=triu[:], scalar1=1.0 / 16, scalar2=-1.0 / 16,
                                              op0=ALU.mult, op1=ALU.add),
             reads=["triu"], writes=["triSg"])
        ones = k.sb([128, 128], F32, "ones")
        k.op("pool", lambda e: e.memset(ones[:], 1.0), writes=["ones"])
        sel63 = k.sb([128, 128], F32, "sel63")
        k.op("pool", lambda e: e.memset(sel63[:], 1.0), writes=["sel63"])
        k.op("pool", lambda e: e.affine_select(out=sel63[:], in_=sel63[:], pattern=[[0, 128]], compare_op=ALU.is_equal,
                                               fill=0.0, base=-63, channel_multiplier=1),
             reads=["sel63"], writes=["sel63"])
        gqT = k.sb([128, L], F32, "gqT")
        gkT = k.sb([128, L], F32, "gkT")
        k.dma("sp", gqT[:], gqT_d, writes=["gqT"])
        k.dma("sp", gkT[:], gkT_d, writes=["gkT"])
        glr = k.sb([17, L], BF16, "glr")
        k.op("pool", lambda e: e.memset(glr[:], 1.0), writes=["glr"])
        k.dma("pool", glr[0:16, :], glrT_d, writes=["glr"])
        aaug = k.sb([17, 128], BF16, "aaug")
        k.dma("pool", aaug[0:16, :], aup_d, writes=["aaug0"])
        k.dma("pool", aaug[16:17, :], ab_d.rearrange("(a n) -> a n", a=1), writes=["aaug1"])
        AAUG = ["aaug0", "aaug1"]
        gv = k.sb([128, NBLK, 256], BF16, "gv")
        k.dma("sp", gv[:], gv_d.rearrange("(blk p) d -> p blk d", p=128), writes=["gv"])
        hgb = k.sb([128, 256], F32, "hgb")
        k.dma("sp", hgb[:], hg_d.partition_broadcast(128), writes=["hgb"])
        fqT = k.sb([128, 2, L], BF16, "fqT")
        fkT = k.sb([128, 2, L], BF16, "fkT")
        k.dma("sp", fqT[:], fqT_d.rearrange("(h d) t -> d h t", h=2), writes=["fqT"])
        k.dma("sp", fkT[:], fkT_d.rearrange("(h d) t -> d h t", h=2), writes=["fkT"])
        vaug = k.sb([128, 2, NBLK, 129], BF16, "vaug")
        k.op("pool", lambda e: e.memset(vaug[:], 1.0), writes=["vaug"])
        for h in range(2):
            k.dma("sp", vaug[:, h, :, 0:128],
                  fv_d.rearrange("(blk p) (h d) -> p h blk d", p=128, h=2)[:, h, :, :], writes=["vaug"])
        S32 = k.sb([128, 256], F32, "S32")
        S16 = k.sb([128, 256], BF16, "S16")
        k.op("dve", lambda e: e.memset(S32[:], 0.0), writes=["S32"])
        k.op("dve", lambda e: e.memset(S16[:], 0.0), writes=["S16"])
        bankG1 = k.ps([128, 512], F32, "bankG1")
        bankG2 = k.ps([128, 512], F32, "bankG2")
        psS = Rot(k, 2, [128, 512], F32, psum=True, name="psS")
        psO = Rot(k, 2, [128, 512], F32, psum=True, name="psO")
        Ccol = k.sb([128, 2, NBLK], F32, "Ccol")
        nCref = k.sb([128, 2, NBLK], F32, "nCref")
        wtab = k.sb([128, 2, 528], F32, "wtab")
        ffc = k.sb([128, 2, NBLK], F32, "ffc")
        k.dma("sp", ffc[:], ff_d.rearrange("h p b -> p h b"), writes=["ffc"])
        nfb = k.sb([128, 2], F32, "nfb")
        k.dma("sp", nfb[:], fb_d.partition_broadcast(128), writes=["nfb"])
        k.op("dve", lambda e: e.tensor_scalar(out=nfb[:], in0=nfb[:], scalar1=-1.0, scalar2=None, op0=ALU.mult),
             reads=["nfb"], writes=["nfb"])
        Lf = k.sb([128, 2, NBLK], F32, "Lf")
        tmpc = k.sb([128, NBLK], F32, "tmpc")
        incl = k.sb([128, NBLK], F32, "incl")
        for h in range(2):
            k.op("act", lambda e, h=h: e.activation(out=Lf[:, h, :], in_=ffc[:, h, :], func=AF.Exp, scale=-1.0,
                                                    bias=nfb[:, h:h + 1]),
                 reads=["ffc", "nfb"], writes=["Lf"])
            k.op("act", lambda e, h=h: e.activation(out=Lf[:, h, :], in_=Lf[:, h, :], func=AF.Ln, bias=1.0),
                 reads=["Lf"], writes=["Lf"])
            k.op("pe", lambda e, h=h: e.matmul(bankG1[:, 0:NBLK], triu[:], Lf[:, h, :], start=True, stop=True),
                 reads=["triu", "Lf"], writes=["bankG1"])
            k.op("pe", lambda e, h=h: e.matmul(bankG1[:, 128:128 + NBLK], ones[:], Lf[:, h, :], start=True, stop=True),
                 reads=["ones", "Lf"], writes=["bankG1"])
            k.op("dve", lambda e: e.tensor_tensor_scan(out=incl[:], data0=ones[:, 0:NBLK], data1=bankG1[:, 128:128 + NBLK],
                                                       initial=0.0, op0=ALU.mult, op1=ALU.add),
                 reads=["ones", "bankG1"], writes=["incl"])
            k.op("dve", lambda e: e.tensor_tensor(out=tmpc[:], in0=incl[:], in1=bankG1[:, 128:128 + NBLK], op=ALU.subtract),
                 reads=["incl", "bankG1"], writes=["tmpc"])
            k.op("dve", lambda e, h=h: e.tensor_tensor(out=Ccol[:, h, :], in0=tmpc[:], in1=bankG1[:, 0:NBLK], op=ALU.add),
                 reads=["tmpc", "bankG1"], writes=["Ccol"])
            k.op("pe", lambda e, h=h: e.matmul(bankG1[:, 256:256 + NBLK], sel63[:], Ccol[:, h, :], start=True, stop=True),
                 reads=["sel63", "Ccol"], writes=["bankG1"])
            k.op("dve", lambda e, h=h: e.tensor_scalar(out=nCref[:, h, :], in0=bankG1[:, 256:256 + NBLK], scalar1=-1.0,
                                                       scalar2=None, op0=ALU.mult),
                 reads=["bankG1"], writes=["nCref"])
            for I in range(NBLK):
                off = I * (I + 1) // 2
                k.op("act", lambda e, h=h, I=I, off=off: e.activation(
                    out=wtab[:, h, off:off + I + 1], in_=Ccol[:, h, 0:I + 1], func=AF.Exp, bias=nCref[:, h, I:I + 1]),
                     reads=["Ccol", "nCref"], writes=["wtab"])
        gkNr = Rot(k, 2, [128, 128], F32, name="gkN")
        ggr = Rot(k, 2, [128, 256], F32, name="ggate")
        fgr = Rot(k, 2, [128, 256], F32, name="fgate")
        lsb = k.sb([128, 128], F32, "lsb")
        ET = k.sb([128, 128], F32, "ET")
        EnT = k.sb([128, 128], F32, "EnT")
        ES = k.sb([128, 128], F32, "ES")
        qtil = k.sb([128, 128], BF16, "qtil")
        ktil = k.sb([128, 128], BF16, "ktil")
        khat = k.sb([128, 128], BF16, "khat")
        AT = k.sb([128, 128], BF16, "AT")
        scr = k.sb([128, 256], F32, "scr")
        ssg = k.sb([128, 1], F32, "ssg")
        rsg = k.sb([128, 1], F32, "rsg")
        sgg = k.sb([128, 256], F32, "sgg")
        o1 = k.sb([128, 256], F32, "o1")
        mixr = Rot(k, 2, [128, 512], F32, name="mixt")
        PTr = Rot(k, 2, [128, 4, 128], BF16, name="PT")
        Vsr = Rot(k, 4, [128, 129], BF16, name="Vs")
        rden = k.sb([128, 1], F32, "rden")
        sgf = k.sb([128, 256], F32, "sgf")
        gkN_v = gkN_d.rearrange("(blk p) d -> blk p d", p=128)
        SC_G = 128 ** -0.5
        SC_F = 128 ** -0.5

        for c in range(NBLK):
            t0, t1 = c * 128, (c + 1) * 128
            mt, mk = mixr.next()
            gkNt, gkNk = gkNr.next()
            k.dma("sp", gkNt[:], gkN_v[c], writes=[gkNk])
            ggt, ggk = ggr.next()
            k.dma("sp", ggt[:], ggate_d[t0:t1, :], writes=[ggk])
            fgt, fgk = fgr.next()
            k.dma("sp", fgt[:], fgate_d[t0:t1, :], writes=[fgk])
            k.op("pe", lambda e, t0=t0, t1=t1: e.matmul(bankG1[:, 0:128], glr[:, t0:t1], aaug[:], start=True, stop=True),
                 reads=["glr"] + AAUG, writes=["bankG1"])
            k.op("act", lambda e: e.activation(out=lsb[:], in_=bankG1[:, 0:128], func=AF.Exp, scale=-1.0),
                 reads=["bankG1"], writes=["lsb"])
            k.op("act", lambda e: e.activation(out=lsb[:], in_=lsb[:], func=AF.Ln, bias=1.0), reads=["lsb"], writes=["lsb"])
            k.op("pe", lambda e: e.matmul(bankG1[:, 128:256], lsb[:], triUg[:], start=True, stop=True),
                 reads=["lsb", "triUg"], writes=["bankG1"])
            k.op("pe", lambda e: e.matmul(bankG1[:, 256:384], triSg[:], lsb[:], start=True, stop=True),
                 reads=["lsb", "triSg"], writes=["bankG1"])
            k.op("act", lambda e: e.activation(out=ET[:], in_=bankG1[:, 128:256], func=AF.Exp), reads=["bankG1"], writes=["ET"])
            k.op("act", lambda e: e.activation(out=EnT[:], in_=bankG1[:, 128:256], func=AF.Exp, scale=-1.0),
                 reads=["bankG1"], writes=["EnT"])
            k.op("act", lambda e: e.activation(out=ES[:], in_=bankG1[:, 256:384], func=AF.Exp), reads=["bankG1"], writes=["ES"])
            k.op("dve", lambda e, t0=t0, t1=t1: e.scalar_tensor_tensor(out=qtil[:], in0=gqT[:, t0:t1], scalar=SC_G, in1=ET[:],
                                                                       op0=ALU.mult, op1=ALU.mult),
                 reads=["gqT", "ET"], writes=["qtil"])
            k.op("dve", lambda e, t0=t0, t1=t1: e.tensor_tensor(out=ktil[:], in0=gkT[:, t0:t1], in1=EnT[:], op=ALU.mult),
                 reads=["gkT", "EnT"], writes=["ktil"])
            k.op("dve", lambda e, gkNt=gkNt: e.tensor_tensor(out=khat[:], in0=gkNt[:], in1=ES[:], op=ALU.mult),
                 reads=[gkNk, "ES"], writes=["khat"])
            k.op("pe", lambda e: e.matmul(bankG1[:, 384:512], ktil[:], qtil[:], start=True, stop=True),
                 reads=["ktil", "qtil"], writes=["bankG1"])
            k.op("dve", lambda e: e.tensor_tensor(out=AT[:], in0=bankG1[:, 384:512], in1=triu[:], op=ALU.mult),
                 reads=["bankG1", "triu"], writes=["AT"])
            k.op("pe", lambda e, c=c: e.matmul(bankG2[:, 0:256], AT[:], gv[:, c, :], start=True, stop=False),
                 reads=["AT", "gv"], writes=["bankG2o"], inc=False)
            k.op("pe", lambda e: e.matmul(bankG2[:, 0:256], qtil[:], S16[:], start=False, stop=True),
                 reads=["qtil", "S16"], writes=["bankG2o"])
            k.op("pe", lambda e, c=c: e.matmul(bankG2[:, 256:512], khat[:], gv[:, c, :], start=True, stop=True),
                 reads=["khat", "gv"], writes=["bankG2s"])
            k.op("dve", lambda e: e.scalar_tensor_tensor(out=S32[:], in0=S32[:], scalar=ET[:, 127:128], in1=bankG2[:, 256:512],
                                                         op0=ALU.mult, op1=ALU.add),
                 reads=["S32", "ET", "bankG2s"], writes=["S32"])
            k.op("act", lambda e: e.activation(out=S16[:], in_=S32[:], func=AF.Copy), reads=["S32"], writes=["S16"])
            k.op("act", lambda e: e.activation(out=scr[:], in_=bankG2[:, 0:256], func=AF.Square, accum_out=ssg[:]),
                 reads=["bankG2o"], writes=["scr", "ssg"])
            k.op("dve", lambda e: e.tensor_scalar(out=rsg[:], in0=ssg[:], scalar1=1.0 / 256, scalar2=EPS, op0=ALU.mult, op1=ALU.add),
                 reads=["ssg"], writes=["rsg"])
            k.op("act", lambda e: e.activation(out=rsg[:], in_=rsg[:], func=AF.Sqrt), reads=["rsg"], writes=["rsg"])
            k.op("dve", lambda e: e.reciprocal(out=rsg[:], in_=rsg[:]), reads=["rsg"], writes=["rsg"])
            k.op("act", lambda e, ggt=ggt: e.activation(out=sgg[:], in_=ggt[:], func=AF.Silu), reads=[ggk], writes=["sgg"])
            k.op("dve", lambda e: e.scalar_tensor_tensor(out=o1[:], in0=bankG2[:, 0:256], scalar=rsg[:, 0:1], in1=hgb[:],
                                                         op0=ALU.mult, op1=ALU.mult),
                 reads=["bankG2o", "rsg", "hgb"], writes=["o1"])
            k.op("dve", lambda e, mt=mt: e.tensor_tensor(out=mt[:, 0:256], in0=o1[:], in1=sgg[:], op=ALU.mult),
                 reads=["o1", "sgg"], writes=[mk])
            I = c
            k.op("act", lambda e, fgt=fgt: e.activation(out=sgf[:], in_=fgt[:], func=AF.Silu), reads=[fgk], writes=["sgf"])
            for h in range(2):
                po, pok = psO.next()
                off = I * (I + 1) // 2
                for J0 in range(0, I + 1, 4):
                    nJ = min(4, I + 1 - J0)
                    pS, pSk = psS.next()
                    for jj in range(nJ):
                        J = J0 + jj
                        k.op("pe", lambda e, pS=pS, jj=jj, J=J, h=h, t0=t0, t1=t1: e.matmul(
                            pS[:, jj * 128:(jj + 1) * 128], fkT[:, h, J * 128:(J + 1) * 128], fqT[:, h, t0:t1],
                            start=True, stop=True),
                             reads=["fkT", "fqT"], writes=[pSk], inc=(jj == nJ - 1))
                    PT, PTk = PTr.next()
                    k.op("act", lambda e, PT=PT, pS=pS, nJ=nJ: e.activation(
                        out=PT[:, 0:nJ, :], in_=pS[:, 0:nJ * 128].rearrange("p (a b) -> p a b", a=nJ), func=AF.Exp, scale=SC_F),
                         reads=[pSk], writes=[PTk])
                    if J0 + nJ - 1 == I:
                        jj = nJ - 1
                        k.op("dve", lambda e, PT=PT, jj=jj: e.tensor_tensor(out=PT[:, jj, :], in0=PT[:, jj, :], in1=triu16[:],
                                                                            op=ALU.mult),
                             reads=[PTk, "triu16"], writes=[PTk])
                    for jj in range(nJ):
                        J = J0 + jj
                        Vs, Vsk = Vsr.next()
                        k.op("dve", lambda e, Vs=Vs, J=J, h=h, off=off: e.tensor_scalar(
                            out=Vs[:], in0=vaug[:, h, J, :], scalar1=wtab[:, h, off + J:off + J + 1], scalar2=None, op0=ALU.mult),
                             reads=["vaug", "wtab"], writes=[Vsk])
                        k.op("pe", lambda e, po=po, PT=PT, jj=jj, Vs=Vs, J=J, I=I: e.matmul(
                            po[:, 0:129], PT[:, jj, :], Vs[:], start=(J == 0), stop=(J == I)),
                             reads=[PTk, Vsk], writes=[pok], inc=True)
                k.op("dve", lambda e, po=po: e.reciprocal(out=rden[:], in_=po[:, 128:129]), reads=[pok], writes=["rden"])
                k.op("dve", lambda e, po=po, mt=mt, h=h: e.scalar_tensor_tensor(
                    out=mt[:, 256 + h * 128:256 + (h + 1) * 128], in0=po[:, 0:128], scalar=rden[:, 0:1],
                    in1=sgf[:, h * 128:(h + 1) * 128], op0=ALU.mult, op1=ALU.mult),
                     reads=[pok, "rden", "sgf"], writes=[mk])
            k.dma("sp", mix_d[t0:t1, :], mt[:], reads=[mk])
        for slot in k.dma_pool:
            if slot[2] is not None:
                k._wait("sp", slot[2])
        k.emit()
    return nc


def wait_all_dma(k, eng="sp"):
    for slot in k.dma_pool:
        if slot[2] is not None:
            k._wait(eng, slot[2])


def load_w(k, wrot, w_d, c0, cw):
    wv = w_d.rearrange("(kc k) c -> k kc c", k=128)
    Wt, wk = wrot.next()
    for qq in range(4):
        k.dma("pool", Wt[:, qq * 4:(qq + 1) * 4, 0:cw], wv[:, qq * 4:(qq + 1) * 4, c0:c0 + cw], writes=[wk])
    return Wt, wk


def build_k3():
    NT = NT1
    nc = bass.Bass("TRN2", target_bir_lowering=False)
    mix_d = nc.dram_tensor("mix", [NT, D], F32, kind="ExternalInput").ap()
    x_d = nc.dram_tensor("x", [NT, D], F32, kind="ExternalInput").ap()
    wo_d = nc.dram_tensor("wo", [D, D], F32, kind="ExternalInput").ap()
    g_d = nc.dram_tensor("g", [D], F32, kind="ExternalInput").ap()
    w_d = nc.dram_tensor("w", [D, 2 * D], F32, kind="ExternalInput").ap()
    x1_d = nc.dram_tensor("x1", [NT, D], F32, kind="ExternalOutput").ap()
    uT_d = nc.dram_tensor("uT", [D, NT], BF16, kind="ExternalOutput").ap()
    gT_d = nc.dram_tensor("gT", [D, NT], F32, kind="ExternalOutput").ap()
    with ExitStack() as st:
        k = KB(nc, st)
        ident = make_ident(k)
        mT = k.sb([128, 16, NT], BF16, "mT")
        hT = k.sb([128, 16, NT], BF16, "hT")
        gT = k.sb([128, 16], F32, "gT")
        k.dma("sp", gT[:], g_d.rearrange("(kc k) -> k kc", k=128), writes=["gT"], allow_slow_non_contiguous=True)
        x1b = k.sb([128, NT // 128, D], F32, "x1b")
        xr = Rot(k, 2, [128, D], F32, name="xt")
        scr = k.sb([128, D], F32, "scr")
        ss = k.sb([128, 1], F32, "ss")
        rstd = k.sb([128, 1], F32, "rstd")
        pst = Rot(k, 2, [128, 512], F32, psum=True, name="pst")
        psr = Rot(k, 4, [128, 512], F32, psum=True, name="psr")
        wrot = Rot(k, 2, [128, 16, 512], BF16, name="Wt")
        stg = {F32: Rot(k, 3, [128, 512], F32, name="stf"), BF16: Rot(k, 3, [128, 512], BF16, name="stb")}
        for tt in range(NT // 128):
            xt, xk = xr.next()
            k.dma("sp", xt[:], mix_d[tt * 128:(tt + 1) * 128, :], writes=[xk])
            k.dma("sp", x1b[:, tt, :], x_d[tt * 128:(tt + 1) * 128, :], writes=["x1b%d" % tt])
            tok_to_feat(k, xt, xk, mT, "mT", tt, ident, pst, evac=("dve" if tt % 2 else "act"))
        for cg in range(4):
            Wt, wk = load_w(k, wrot, wo_d, cg * 512, 512)
            for tt in range(NT // 128):
                pt, pk = psr.next()
                for kc in range(16):
                    k.op("pe", lambda e, pt=pt, kc=kc, tt=tt, Wt=Wt: e.matmul(
                        pt[:], mT[:, kc, tt * 128:(tt + 1) * 128], Wt[:, kc, :], start=(kc == 0), stop=(kc == 15)),
                         reads=[wk, "mT"], writes=[pk], inc=(kc == 15))
                o = x1b[:, tt, cg * 512:(cg + 1) * 512]
                k.op("dve", lambda e, o=o, pt=pt: e.tensor_tensor(out=o, in0=o, in1=pt[:], op=ALU.add),
                     reads=[pk, "x1b%d" % tt], writes=["x1b%d" % tt])
        for tt in range(NT // 128):
            xk = "x1b%d" % tt
            k.dma("sp", x1_d[tt * 128:(tt + 1) * 128, :], x1b[:, tt, :], reads=[xk])
            rms_scale(k, x1b[:, tt, :], xk, scr, ss, rstd, "")
            xt, xk2 = xr.next()
            k.op("dve", lambda e, xt=xt, tt=tt: e.tensor_scalar(out=xt[:], in0=x1b[:, tt, :], scalar1=rstd[:, 0:1],
                                                                scalar2=None, op0=ALU.mult),
                 reads=[xk, "rstd"], writes=[xk2])
            tok_to_feat(k, xt, xk2, hT, "hT", tt, ident, pst, gT=gT)

        def oT(t, off):
            return lambda r0, nr, c0, ncs: t[off + r0:off + r0 + nr, c0:c0 + ncs]

        groups = [(cg * 512, 512, [("T", oT(uT_d, cg * 512), BF16)]) for cg in range(4)] + \
                 [(D + cg * 512, 512, [("T", oT(gT_d, cg * 512), F32)]) for cg in range(4)]
        proj_groups(k, hT, ["hT"], w_d, groups, NT, wrot, psr, stg, [0])
        wait_all_dma(k)
        k.emit()
    return nc


def build_k5():
    NT = NT1
    nc = bass.Bass("TRN2", target_bir_lowering=False)
    yT_d = nc.dram_tensor("yT", [D, NT], F32, kind="ExternalInput").ap()
    gT_d = nc.dram_tensor("gT", [D, NT], F32, kind="ExternalInput").ap()
    x1_d = nc.dram_tensor("x1", [NT, D], F32, kind="ExternalInput").ap()
    wg_d = nc.dram_tensor("wg", [D, D], F32, kind="ExternalInput").ap()
    bg_d = nc.dram_tensor("bg", [D], F32, kind="ExternalInput").ap()
    wo_d = nc.dram_tensor("wo", [D, D], F32, kind="ExternalInput").ap()
    gf_d = nc.dram_tensor("gf", [D], F32, kind="ExternalInput").ap()
    out_d = nc.dram_tensor("out", [NT, D], F32, kind="ExternalOutput").ap()
    with ExitStack() as st:
        k = KB(nc, st)
        z16 = k.sb([128, 16, NT], BF16, "z16")
        z3 = k.sb([128, 16, NT], BF16, "z3")
        bgT = k.sb([128, 16], F32, "bgT")
        k.dma("sp", bgT[:], bg_d.rearrange("(kc k) -> k kc", k=128), writes=["bgT"], allow_slow_non_contiguous=True)
        gfb = k.sb([128, D], F32, "gfb")
        k.dma("sp", gfb[:], gf_d.partition_broadcast(128), writes=["gfb"])
        yr = Rot(k, 2, [128, NT], F32, name="yt")
        tr = Rot(k, 2, [128, NT], F32, name="tt")
        gr = Rot(k, 2, [128, 512], F32, name="gt")
        sr = Rot(k, 2, [128, 512], F32, name="sg")
        psr = Rot(k, 4, [128, 512], F32, psum=True, name="psr")
        wrot = Rot(k, 4, [128, 16, 512], BF16, name="Wt")
        scr = k.sb([128, D], F32, "scr")
        ss = k.sb([128, 1], F32, "ss")
        rstd = k.sb([128, 1], F32, "rstd")
        x2r = Rot(k, 2, [128, D], F32, name="x2")
        yv = yT_d.rearrange("(kc k) t -> kc k t", k=128)
        gv = gT_d.rearrange("(kc k) t -> kc k t", k=128)
        for kc in range(16):
            yt, yk = yr.next()
            k.dma("sp", yt[:], yv[kc], writes=[yk])
            t, tk = tr.next()
            k.op("act", lambda e, t=t, yt=yt: e.activation(out=t[:], in_=yt[:], func=AF.Square), reads=[yk], writes=[tk])
            k.op("dve", lambda e, t=t: e.tensor_scalar(out=t[:], in0=t[:], scalar1=0.044715, scalar2=1.0, op0=ALU.mult,
                                                       op1=ALU.add), reads=[tk], writes=[tk])
            k.op("dve", lambda e, t=t, yt=yt: e.tensor_tensor(out=t[:], in0=t[:], in1=yt[:], op=ALU.mult),
                 reads=[tk, yk], writes=[tk])
            k.op("act", lambda e, t=t: e.activation(out=t[:], in_=t[:], func=AF.Sigmoid, scale=1.5957691216057308),
                 reads=[tk], writes=[tk])
            k.op("dve", lambda e, t=t, yt=yt, kc=kc: e.tensor_tensor(out=z16[:, kc, :], in0=t[:], in1=yt[:], op=ALU.mult),
                 reads=[tk, yk], writes=["z16"])
        for cg in range(4):
            Wt, wk = load_w(k, wrot, wg_d, cg * 512, 512)
            for m in range(4):
                kco = cg * 4 + m
                for th in range(NT // 512):
                    pt, pk = psr.next()
                    for kc in range(16):
                        k.op("pe", lambda e, pt=pt, kc=kc, m=m, th=th, Wt=Wt: e.matmul(
                            pt[:], Wt[:, kc, m * 128:(m + 1) * 128], z16[:, kc, th * 512:(th + 1) * 512],
                            start=(kc == 0), stop=(kc == 15)),
                             reads=[wk, "z16"], writes=[pk], inc=(kc == 15))
                    gt, gk = gr.next()
                    k.dma("sp", gt[:], gv[kco][:, th * 512:(th + 1) * 512], writes=[gk])
                    sg, sk = sr.next()
                    k.op("act", lambda e, sg=sg, pt=pt, kco=kco: e.activation(out=sg[:], in_=pt[:], func=AF.Sigmoid,
                                                                              bias=bgT[:, kco:kco + 1]),
                         reads=[pk, "bgT"], writes=[sk])
                    k.op("act", lambda e, gt=gt: e.activation(out=gt[:], in_=gt[:], func=AF.Silu), reads=[gk], writes=[gk])
                    k.op("dve", lambda e, sg=sg, kco=kco, th=th: e.tensor_tensor(
                        out=sg[:], in0=sg[:], in1=z16[:, kco, th * 512:(th + 1) * 512], op=ALU.mult),
                         reads=[sk, "z16"], writes=[sk])
                    k.op("dve", lambda e, sg=sg, gt=gt, kco=kco, th=th: e.tensor_tensor(
                        out=z3[:, kco, th * 512:(th + 1) * 512], in0=sg[:], in1=gt[:], op=ALU.mult),
                         reads=[sk, gk], writes=["z3"])
        wres, wresk = [], []
        for cg in range(4):
            Wt, wk = load_w(k, wrot, wo_d, cg * 512, 512)
            wres.append(Wt)
            wresk.append(wk)
        for tt in range(NT // 128):
            x2, x2k = x2r.next()
            k.dma("sp", x2[:], x1_d[tt * 128:(tt + 1) * 128, :], writes=[x2k])
            for cg in range(4):
                pt, pk = psr.next()
                for kc in range(16):
                    k.op("pe", lambda e, pt=pt, kc=kc, tt=tt, cg=cg: e.matmul(
                        pt[:], z3[:, kc, tt * 128:(tt + 1) * 128], wres[cg][:, kc, :], start=(kc == 0), stop=(kc == 15)),
                         reads=[wresk[cg], "z3"], writes=[pk], inc=(kc == 15))
                o = x2[:, cg * 512:(cg + 1) * 512]
                k.op("dve", lambda e, o=o, pt=pt: e.tensor_tensor(out=o, in0=o, in1=pt[:], op=ALU.add),
                     reads=[pk, x2k], writes=[x2k])
            rms_scale(k, x2, x2k, scr, ss, rstd, "")
            k.op("dve", lambda e, x2=x2: e.scalar_tensor_tensor(out=x2[:], in0=x2[:], scalar=rstd[:, 0:1], in1=gfb[:],
                                                                op0=ALU.mult, op1=ALU.mult),
                 reads=[x2k, "rstd", "gfb"], writes=[x2k])
            k.dma("sp", out_d[tt * 128:(tt + 1) * 128, :], x2[:], reads=[x2k])
        wait_all_dma(k)
        k.emit()
    return nc


TC = 16
NCH = 256
NCOL = 2 * NCH


def build_k4():
    nc = bass.Bass("TRN2", target_bir_lowering=False)
    din = lambda n, s, d=F32: nc.dram_tensor(n, s, d, kind="ExternalInput").ap()
    u_d = din("u", [2, 128, TC, NCOL], BF16)
    lamre_d = din("lamre", [128, 8])
    lamim_d = din("lamim", [128, 8])
    lstep_d = din("lstep", [128, 8])
    bre_d = din("bre", [128, 8, 16])
    bim_d = din("bim", [128, 8, 16])
    cre_d = din("cre", [128, 8, 16])
    cim_d = din("cim", [128, 8, 16])
    dT_d = din("dT", [128, 2])
    y_d = nc.dram_tensor("y", [2, 128, NCOL, TC], F32, kind="ExternalOutput").ap()
    TWO_PI = 6.283185307179586
    with ExitStack() as st:
        k = KB(nc, st)
        ident = make_ident(k)
        cnt = [0]

        def T(shape, dt=F32):
            cnt[0] += 1
            name = "v%d" % cnt[0]
            return k.sb(shape, dt, name), name

        def ld(d, shape):
            t, n = T(shape)
            k.dma("sp", t[:], d, writes=[n])
            return t, n

        def tt(o, on, a, an, b, bn, op, eng="dve"):
            k.op(eng, lambda e: e.tensor_tensor(out=o, in0=a, in1=b, op=op), reads=[an, bn], writes=[on])

        def ts(o, on, a, an, s1, s2=None, op0=ALU.mult, op1=None, extra=()):
            if op1 is None:
                k.op("dve", lambda e: e.tensor_scalar(out=o, in0=a, scalar1=s1, scalar2=None, op0=op0),
                     reads=[an] + list(extra), writes=[on])
            else:
                k.op("dve", lambda e: e.tensor_scalar(out=o, in0=a, scalar1=s1, scalar2=s2, op0=op0, op1=op1),
                     reads=[an] + list(extra), writes=[on])

        def act(o, on, a, an, func, scale=None, bias=None):
            kw = {}
            if scale is not None:
                kw["scale"] = scale
            if bias is not None:
                kw["bias"] = bias
            k.op("act", lambda e: e.activation(out=o, in_=a, func=func, **kw), reads=[an], writes=[on])

        lamre, n_lamre = ld(lamre_d, [128, 8])
        lamim, n_lamim = ld(lamim_d, [128, 8])
        lstep, n_lstep = ld(lstep_d, [128, 8])
        bre, n_bre = ld(bre_d, [128, 8, 16])
        bim, n_bim = ld(bim_d, [128, 8, 16])
        cre, n_cre = ld(cre_d, [128, 8, 16])
        cim, n_cim = ld(cim_d, [128, 8, 16])
        dT, n_dT = ld(dT_d, [128, 2])
        u = k.sb([128, 2, TC, NCOL], BF16, "u")
        for kb in range(2):
            k.dma("sp", u[:, kb, :, :], u_d[kb], writes=["u%d" % kb])
        m0, n_m0 = T([128, 1])
        m1, n_m1 = T([128, 1])
        k.op("pool", lambda e: e.memset(m0[:], 0.0), writes=[n_m0])
        k.op("pool", lambda e: e.memset(m0[0:64, :], 1.0), writes=[n_m0])
        k.op("pool", lambda e: e.memset(m1[:], 1.0), writes=[n_m1])
        k.op("pool", lambda e: e.memset(m1[0:64, :], 0.0), writes=[n_m1])
        lr, n_lr = T([128, 8])
        ts(lr[:], n_lr, lamre[:], n_lamre, -1e-4, op0=ALU.min)
        stp, n_stp = T([128, 8])
        act(stp[:], n_stp, lstep[:], n_lstep, AF.Exp)
        lrs, n_lrs = T([128, 8])
        tt(lrs[:], n_lrs, lr[:], n_lr, stp[:], n_stp, ALU.mult)
        mag, n_mag = T([128, 8])
        act(mag[:], n_mag, lrs[:], n_lrs, AF.Exp)
        rhoc, n_rhoc = T([128, 8])
        act(rhoc[:], n_rhoc, lrs[:], n_lrs, AF.Exp, scale=float(TC))
        ang, n_ang = T([128, 8])
        tt(ang[:], n_ang, lamim[:], n_lamim, stp[:], n_stp, ALU.mult)

        def sin_of(turn_offset):
            x, nx = T([128, 8])
            ts(x[:], nx, ang[:], n_ang, 1.0 / TWO_PI, turn_offset, op0=ALU.mult, op1=ALU.add)
            xi, nxi = T([128, 8], mybir.dt.int32)
            k.op("dve", lambda e: e.tensor_copy(out=xi[:], in_=x[:]), reads=[nx], writes=[nxi])
            xf, nxf = T([128, 8])
            k.op("dve", lambda e: e.tensor_copy(out=xf[:], in_=xi[:]), reads=[nxi], writes=[nxf])
            fr, nfr = T([128, 8])
            tt(fr[:], nfr, x[:], nx, xf[:], nxf, ALU.subtract)
            mk, nmk = T([128, 8])
            ts(mk[:], nmk, fr[:], nfr, 0.5, op0=ALU.is_gt)
            tt(fr[:], nfr, fr[:], nfr, mk[:], nmk, ALU.subtract)
            ts(mk[:], nmk, fr[:], nfr, -0.5, op0=ALU.is_lt)
            tt(fr[:], nfr, fr[:], nfr, mk[:], nmk, ALU.add)
            s, ns = T([128, 8])
            act(s[:], ns, fr[:], nfr, AF.Sin, scale=TWO_PI)
            return s, ns

        sn, n_sn = sin_of(0.0)
        cs, n_cs = sin_of(0.25)
        LPr, n_LPr = T([128, 17, 8])
        LPi, n_LPi = T([128, 17, 8])
        k.op("dve", lambda e: e.memset(LPr[:, 0, :], 1.0), writes=[n_LPr])
        k.op("dve", lambda e: e.memset(LPi[:, 0, :], 0.0), writes=[n_LPi])
        tt(LPr[:, 1, :], n_LPr, mag[:], n_mag, cs[:], n_cs, ALU.mult)
        tt(LPi[:, 1, :], n_LPi, mag[:], n_mag, sn[:], n_sn, ALU.mult)
        t1, n_t1 = T([128, 8])
        t2, n_t2 = T([128, 8])
        for tau in range(2, 17):
            tt(t1[:], n_t1, LPr[:, tau - 1, :], n_LPr, LPr[:, 1, :], n_LPr, ALU.mult)
            tt(t2[:], n_t2, LPi[:, tau - 1, :], n_LPi, LPi[:, 1, :], n_LPi, ALU.mult)
            tt(LPr[:, tau, :], n_LPr, t1[:], n_t1, t2[:], n_t2, ALU.subtract)
            tt(t1[:], n_t1, LPr[:, tau - 1, :], n_LPr, LPi[:, 1, :], n_LPi, ALU.mult)
            tt(t2[:], n_t2, LPi[:, tau - 1, :], n_LPi, LPr[:, 1, :], n_LPr, ALU.mult)
            tt(LPi[:, tau, :], n_LPi, t1[:], n_t1, t2[:], n_t2, ALU.add)
        den, n_den = T([128, 8])
        tt(t1[:], n_t1, lr[:], n_lr, lr[:], n_lr, ALU.mult)
        tt(t2[:], n_t2, lamim[:], n_lamim, lamim[:], n_lamim, ALU.mult)
        tt(den[:], n_den, t1[:], n_t1, t2[:], n_t2, ALU.add)
        k.op("dve", lambda e: e.reciprocal(out=den[:], in_=den[:]), reads=[n_den], writes=[n_den])
        nr, n_nr = T([128, 8])
        ts(nr[:], n_nr, LPr[:, 1, :], n_LPr, -1.0, op0=ALU.add)
        cfr, n_cfr = T([128, 8])
        cfi, n_cfi = T([128, 8])
        tt(t1[:], n_t1, nr[:], n_nr, lr[:], n_lr, ALU.mult)
        tt(t2[:], n_t2, LPi[:, 1, :], n_LPi, lamim[:], n_lamim, ALU.mult)
        tt(cfr[:], n_cfr, t1[:], n_t1, t2[:], n_t2, ALU.add)
        tt(cfr[:], n_cfr, cfr[:], n_cfr, den[:], n_den, ALU.mult)
        tt(t1[:], n_t1, LPi[:, 1, :], n_LPi, lr[:], n_lr, ALU.mult)
        tt(t2[:], n_t2, nr[:], n_nr, lamim[:], n_lamim, ALU.mult)
        tt(cfi[:], n_cfi, t1[:], n_t1, t2[:], n_t2, ALU.subtract)
        tt(cfi[:], n_cfi, cfi[:], n_cfi, den[:], n_den, ALU.mult)
        Bre, n_Bre = T([128, 8, 16])
        Bim, n_Bim = T([128, 8, 16])
        w1, n_w1 = T([128, 8, 16])
        w2, n_w2 = T([128, 8, 16])
        bc = lambda a: a.unsqueeze(2).to_broadcast([128, 8, 16])
        tt(w1[:], n_w1, bre[:], n_bre, bc(cfr[:]), n_cfr, ALU.mult)
        tt(w2[:], n_w2, bim[:], n_bim, bc(cfi[:]), n_cfi, ALU.mult)
        tt(Bre[:], n_Bre, w1[:], n_w1, w2[:], n_w2, ALU.subtract)
        tt(w1[:], n_w1, bim[:], n_bim, bc(cfr[:]), n_cfr, ALU.mult)
        tt(w2[:], n_w2, bre[:], n_bre, bc(cfi[:]), n_cfi, ALU.mult)
        tt(Bim[:], n_Bim, w1[:], n_w1, w2[:], n_w2, ALU.add)
        GPr16 = k.sb([128, 17, 8, 32], BF16, "GPr16")
        nGPi16 = k.sb([128, 17, 8, 32], BF16, "nGPi16")
        GPr, n_GPr, nGPi, n_nGPi = GPr16, "GPr16", nGPi16, "nGPi16"
        g1, n_g1 = T([128, 17, 8, 16])
        g2, n_g2 = T([128, 17, 8, 16])
        g3, n_g3 = T([128, 17, 8, 16])
        bcC = lambda a: a.unsqueeze(1).to_broadcast([128, 17, 8, 16])
        bcL = lambda a: a.unsqueeze(3).to_broadcast([128, 17, 8, 16])
        tt(g1[:], n_g1, bcC(cre[:]), n_cre, bcL(LPr[:]), n_LPr, ALU.mult)
        tt(g2[:], n_g2, bcC(cim[:]), n_cim, bcL(LPi[:]), n_LPi, ALU.mult)
        tt(g3[:], n_g3, g1[:], n_g1, g2[:], n_g2, ALU.subtract)
        ts(GPr[:, :, :, 0:16], n_GPr, g3[:], n_g3, m0[:, 0:1], extra=[n_m0])
        ts(GPr[:, :, :, 16:32], n_GPr, g3[:], n_g3, m1[:, 0:1], extra=[n_m1])
        tt(g1[:], n_g1, bcC(cre[:]), n_cre, bcL(LPi[:]), n_LPi, ALU.mult)
        tt(g2[:], n_g2, bcC(cim[:]), n_cim, bcL(LPr[:]), n_LPr, ALU.mult)
        tt(g3[:], n_g3, g1[:], n_g1, g2[:], n_g2, ALU.add)
        ts(g3[:], n_g3, g3[:], n_g3, -1.0)
        ts(nGPi[:, :, :, 0:16], n_nGPi, g3[:], n_g3, m0[:, 0:1], extra=[n_m0])
        ts(nGPi[:, :, :, 16:32], n_nGPi, g3[:], n_g3, m1[:, 0:1], extra=[n_m1])
        BPr, n_BPr = T([128, 8, 128], BF16)
        BPi, n_BPi = T([128, 8, 128], BF16)
        k.op("pool", lambda e: e.memset(BPr[:], 0.0), writes=[n_BPr])
        k.op("pool", lambda e: e.memset(BPi[:], 0.0), writes=[n_BPi])
        for pr in range(8):
            pi = pr % 4
            for (src, nsrc, dst, ndst) in ((Bre, n_Bre, BPr, n_BPr), (Bim, n_Bim, BPi, n_BPi)):
                ts(dst[:, pr, 32 * pi:32 * pi + 16], ndst, src[:, pr, :], nsrc, m0[:, 0:1], extra=[n_m0])
                ts(dst[:, pr, 32 * pi + 16:32 * pi + 32], ndst, src[:, pr, :], nsrc, m1[:, 0:1], extra=[n_m1])
        KT = k.sb([128, 2, TC, 128], BF16, "KT")
        pprep = Rot(k, 2, [128, 512], F32, psum=True, name="pprep")
        for kb in range(2):
            for tau in range(TC):
                pt, pk = pprep.next()
                for pi in range(4):
                    pr = kb * 4 + pi
                    k.op("pe", lambda e, pt=pt, pi=pi, pr=pr, tau=tau: e.matmul(
                        pt[:, 32 * pi:32 * pi + 32], BPr[:, pr, :], GPr[:, tau, pr, :], start=True, stop=False),
                         reads=[n_BPr, n_GPr], writes=[pk], inc=False)
                    k.op("pe", lambda e, pt=pt, pi=pi, pr=pr, tau=tau: e.matmul(
                        pt[:, 32 * pi:32 * pi + 32], BPi[:, pr, :], nGPi[:, tau, pr, :], start=False, stop=True),
                         reads=[n_BPi, n_nGPi], writes=[pk], inc=(pi == 3))
                if tau == 0:
                    k.op("dve", lambda e, pt=pt, kb=kb: e.scalar_tensor_tensor(
                        out=KT[:, kb, 0, :], in0=ident[:], scalar=dT[:, kb:kb + 1], in1=pt[:, 0:128],
                        op0=ALU.mult, op1=ALU.add), reads=["ident", n_dT, pk], writes=["KT"])
                else:
                    k.op("dve", lambda e, pt=pt, kb=kb, tau=tau: e.tensor_copy(out=KT[:, kb, tau, :], in_=pt[:, 0:128]),
                         reads=[pk], writes=["KT"])
        MBTr = k.sb([128, 2, TC, 128], BF16, "MBTr")
        MBTi = k.sb([128, 2, TC, 128], BF16, "MBTi")
        XZ, n_XZ = T([128, TC // 2, 8, 32])
        mm1, n_mm1 = g1[:, 0:TC], n_g1
        mm2, n_mm2 = g2[:, 0:TC], n_g2
        mm3, n_mm3 = g3[:, 0:TC], n_g3
        LRr, n_LRr = T([128, TC, 8])
        LRi, n_LRi = T([128, TC, 8])
        for s in range(TC):
            k.op("dve", lambda e, s=s: e.tensor_copy(out=LRr[:, s, :], in_=LPr[:, TC - 1 - s, :]), reads=[n_LPr], writes=[n_LRr])
            k.op("dve", lambda e, s=s: e.tensor_copy(out=LRi[:, s, :], in_=LPi[:, TC - 1 - s, :]), reads=[n_LPi], writes=[n_LRi])
        bcB = lambda a: a.unsqueeze(1).to_broadcast([128, TC, 8, 16])
        bcP = lambda a: a.unsqueeze(3).to_broadcast([128, TC, 8, 16])
        for part, dst in (("re", MBTr), ("im", MBTi)):
            if part == "re":
                tt(mm1[:], n_mm1, bcB(Bre[:]), n_Bre, bcP(LRr[:]), n_LRr, ALU.mult)
                tt(mm2[:], n_mm2, bcB(Bim[:]), n_Bim, bcP(LRi[:]), n_LRi, ALU.mult)
                tt(mm3[:], n_mm3, mm1[:], n_mm1, mm2[:], n_mm2, ALU.subtract)
            else:
                tt(mm1[:], n_mm1, bcB(Bim[:]), n_Bim, bcP(LRr[:]), n_LRr, ALU.mult)
                tt(mm2[:], n_mm2, bcB(Bre[:]), n_Bre, bcP(LRi[:]), n_LRi, ALU.mult)
                tt(mm3[:], n_mm3, mm1[:], n_mm1, mm2[:], n_mm2, ALU.add)
            for hf in range(2):
                h0 = hf * (TC // 2)
                ts(XZ[:, :, :, 0:16], n_XZ, mm3[:, h0:h0 + TC // 2], n_mm3, m0[:, 0:1], extra=[n_m0])
                ts(XZ[:, :, :, 16:32], n_XZ, mm3[:, h0:h0 + TC // 2], n_mm3, m1[:, 0:1], extra=[n_m1])
                for kb in range(2):
                    for s4 in range(TC // 8):
                        pt, pk = pprep.next()
                        for j in range(4):
                            s = s4 * 4 + j
                            k.op("pe", lambda e, pt=pt, j=j, s=s, kb=kb: e.transpose(
                                out=pt[:, j * 128:(j + 1) * 128],
                                in_=XZ[:, s, kb * 4:(kb + 1) * 4, :], identity=ident[:]),
                                 reads=[n_XZ, "ident"], writes=[pk], inc=(j == 3))
                        k.op("dve", lambda e, pt=pt, dst=dst, kb=kb, s4=s4, h0=h0: e.tensor_copy(
                            out=dst[:, kb, h0 + s4 * 4:h0 + (s4 + 1) * 4, :], in_=pt[:].rearrange("p (a b) -> p a b", a=4)),
                             reads=[pk], writes=["MBT" + part])
        rrho, n_rrho = T([128, 8])
        k.op("dve", lambda e: e.reciprocal(out=rrho[:], in_=rhoc[:]), reads=[n_rhoc], writes=[n_rrho])
        CS = k.sb([128, 8, NCH], F32, "CS")
        SN = k.sb([128, 8, NCH], F32, "SN")
        k.op("dve", lambda e: e.memset(CS[:, :, 0:1], 1.0), writes=["CS"])
        k.op("dve", lambda e: e.memset(SN[:, :, 0:1], 0.0), writes=["SN"])
        Ur, n_Ur = T([128, 8])
        Ui, n_Ui = T([128, 8])
        tt(Ur[:], n_Ur, LPr[:, TC, :], n_LPr, rrho[:], n_rrho, ALU.mult)
        tt(Ui[:], n_Ui, LPi[:, TC, :], n_LPi, rrho[:], n_rrho, ALU.mult)
        e1, n_e1 = g1[:, 0:8].rearrange("p a b c -> p a (b c)"), n_g1
        e2, n_e2 = g2[:, 0:8].rearrange("p a b c -> p a (b c)"), n_g2
        m = 1
        while m < NCH:
            bcu = lambda a, m=m: a.unsqueeze(2).to_broadcast([128, 8, m])
            tt(e1[:, :, 0:m], n_e1, CS[:, :, 0:m], "CS", bcu(Ur[:]), n_Ur, ALU.mult)
            tt(e2[:, :, 0:m], n_e2, SN[:, :, 0:m], "SN", bcu(Ui[:]), n_Ui, ALU.mult)
            tt(CS[:, :, m:2 * m], "CS", e1[:, :, 0:m], n_e1, e2[:, :, 0:m], n_e2, ALU.subtract)
            tt(e1[:, :, 0:m], n_e1, CS[:, :, 0:m], "CS", bcu(Ui[:]), n_Ui, ALU.mult)
            tt(e2[:, :, 0:m], n_e2, SN[:, :, 0:m], "SN", bcu(Ur[:]), n_Ur, ALU.mult)
            tt(SN[:, :, m:2 * m], "SN", e1[:, :, 0:m], n_e1, e2[:, :, 0:m], n_e2, ALU.add)
            tt(t1[:], n_t1, Ur[:], n_Ur, Ur[:], n_Ur, ALU.mult)
            tt(t2[:], n_t2, Ui[:], n_Ui, Ui[:], n_Ui, ALU.mult)
            tt(w1[:, :, 0], n_w1, Ur[:], n_Ur, Ui[:], n_Ui, ALU.mult)
            tt(Ur[:], n_Ur, t1[:], n_t1, t2[:], n_t2, ALU.subtract)
            ts(Ui[:], n_Ui, w1[:, :, 0], n_w1, 2.0)
            m *= 2
        Atab = k.sb([128, 2, NCH], F32, "Atab")
        Xr = k.sb([128, 8, 2, NCH + 1], BF16, "Xr")
        Xi = k.sb([128, 8, 2, NCH + 1], BF16, "Xi")
        k.op("pool", lambda e: e.memset(Xr[:], 0.0), writes=["Xr"])
        k.op("pool", lambda e: e.memset(Xi[:], 0.0), writes=["Xi"])
        psz = Rot(k, 2, [128, 512], F32, psum=True, name="psz")
        psy = Rot(k, 2, [128, 512], F32, psum=True, name="psy")
        za, n_za = T([128, 2, NCH])
        zb, n_zb = T([128, 2, NCH])
        zc, n_zc = T([128, 2, NCH])
        zr, n_zr = T([128, 2, NCH])
        zi, n_zi = T([128, 2, NCH])
        wr, n_wr = T([128, 2, NCH])
        wi, n_wi = T([128, 2, NCH])
        yst = Rot(k, 1, [128, NCOL, TC], F32, name="yst")
        for kb in range(2):
            for pi in range(4):
                pr = kb * 4 + pi
                pzr, pzrk = psz.next()
                pzi, pzik = psz.next()
                for (pz, pzk, MBT, part) in ((pzr, pzrk, MBTr, "re"), (pzi, pzik, MBTi, "im")):
                    for s in range(TC):
                        k.op("pe", lambda e, pz=pz, MBT=MBT, s=s, kb=kb, pi=pi: e.matmul(
                            pz[:], MBT[32 * pi:32 * pi + 32, kb, s, :], u[32 * pi:32 * pi + 32, kb, s, :],
                            start=(s == 0), stop=(s == TC - 1), tile_position=(32 * pi, 0)),
                             reads=["MBT" + part, "u%d" % kb], writes=[pzk], inc=(s == TC - 1))
                csb = CS[:, pr, :].unsqueeze(1).to_broadcast([128, 2, NCH])
                snb = SN[:, pr, :].unsqueeze(1).to_broadcast([128, 2, NCH])
                v3 = lambda a: a.rearrange("p (b n) -> p b n", b=2)
                tt(za[:], n_za, v3(pzr[:]), pzrk, csb, "CS", ALU.mult)
                tt(zb[:], n_zb, v3(pzi[:]), pzik, snb, "SN", ALU.mult)
                tt(zr[:], n_zr, za[:], n_za, zb[:], n_zb, ALU.add)
                tt(za[:], n_za, v3(pzi[:]), pzik, csb, "CS", ALU.mult)
                tt(zb[:], n_zb, v3(pzr[:]), pzrk, snb, "SN", ALU.mult)
                tt(zi[:], n_zi, za[:], n_za, zb[:], n_zb, ALU.subtract)
                fl = lambda a: a.rearrange("p b n -> p (b n)")
                k.op("dve", lambda e, pr=pr: e.tensor_copy(out=Atab[:], in_=rhoc[:, pr:pr + 1].unsqueeze(2).to_broadcast([128, 2, NCH])),
                     reads=[n_rhoc], writes=["Atab"])
                k.op("dve", lambda e: e.memset(Atab[:, :, 0:1], 0.0), reads=["Atab"], writes=["Atab"])
                k.op("dve", lambda e: e.tensor_tensor_scan(out=fl(wr[:]), data0=fl(Atab[:]), data1=fl(zr[:]),
                                                           initial=0.0, op0=ALU.mult, op1=ALU.add),
                     reads=["Atab", n_zr], writes=[n_wr])
                k.op("dve", lambda e: e.tensor_tensor_scan(out=fl(wi[:]), data0=fl(Atab[:]), data1=fl(zi[:]),
                                                           initial=0.0, op0=ALU.mult, op1=ALU.add),
                     reads=["Atab", n_zi], writes=[n_wi])
                tt(za[:], n_za, wr[:], n_wr, csb, "CS", ALU.mult)
                tt(zb[:], n_zb, wi[:], n_wi, snb, "SN", ALU.mult)
                tt(Xr[:, pr, :, 1:NCH + 1], "Xr", za[:], n_za, zb[:], n_zb, ALU.subtract)
                tt(za[:], n_za, wi[:], n_wi, csb, "CS", ALU.mult)
                tt(zc[:], n_zc, wr[:], n_wr, snb, "SN", ALU.mult)
                tt(Xi[:, pr, :, 1:NCH + 1], "Xi", za[:], n_za, zc[:], n_zc, ALU.add)
            ys, ysk = yst.next()
            for t_s in range(TC):
                py, pyk = psy.next()
                for s in range(t_s + 1):
                    k.op("pe", lambda e, py=py, kb=kb, t_s=t_s, s=s: e.matmul(
                        py[:], KT[:, kb, t_s - s, :], u[:, kb, s, :], start=(s == 0), stop=False),
                         reads=["KT", "u%d" % kb], writes=[pyk], inc=False)
                for pi in range(4):
                    pr = kb * 4 + pi
                    k.op("pe", lambda e, py=py, pi=pi, pr=pr, t_s=t_s: e.matmul(
                        py[32 * pi:32 * pi + 32, :].rearrange("p (b n) -> p b n", b=2), GPr16[:, t_s + 1, pr, :],
                        Xr[:, pr, :, 0:NCH], start=False, stop=False, tile_position=(0, 32 * pi)),
                         reads=["GPr16", "Xr"], writes=[pyk], inc=False)
                    k.op("pe", lambda e, py=py, pi=pi, pr=pr, t_s=t_s: e.matmul(
                        py[32 * pi:32 * pi + 32, :].rearrange("p (b n) -> p b n", b=2), nGPi16[:, t_s + 1, pr, :],
                        Xi[:, pr, :, 0:NCH], start=False, stop=(pi == 3), tile_position=(0, 32 * pi)),
                         reads=["nGPi16", "Xi"], writes=[pyk], inc=(pi == 3))
                if t_s % 2 == 0:
                    k.op("act", lambda e, py=py, ys=ys, t_s=t_s: e.activation(out=ys[:, :, t_s], in_=py[:], func=AF.Copy),
                         reads=[pyk], writes=[ysk])
                else:
                    k.op("dve", lambda e, py=py, ys=ys, t_s=t_s: e.tensor_copy(out=ys[:, :, t_s], in_=py[:]),
                         reads=[pyk], writes=[ysk])
            for hh in range(4):
                k.dma("sp", y_d[kb][:, hh * 128:(hh + 1) * 128, :], ys[:, hh * 128:(hh + 1) * 128, :], reads=[ysk])
        wait_all_dma(k)
        k.emit()
    return nc


_c = np.ascontiguousarray


def _run(nc, in_maps):
    res = run_bass_kernel_spmd(nc, in_maps, core_ids=list(range(NCORES)))
    return res.results


def _pairP(a):
    return _c(a.reshape(8, 2, 64).transpose(1, 2, 0).reshape(128, 8))


def _pairB(a):
    return _c(a.reshape(8, 2, 64, 16).transpose(1, 2, 0, 3).reshape(128, 8, 16))


def kernel(x, ab_norm_g, ab_w_in, gla_alpha_up, gla_alpha_b, gla_head_g, fox_f_b, ab_w_out,
           c_norm_g, c_w_in, s5_lambda_re, s5_lambda_im, s5_log_step, s5_b_re, s5_b_im,
           s5_c_re, s5_c_im, s5_d, glu_w, glu_b, c_w_out, final_norm_g):
    f32 = np.float32
    x = np.asarray(x, f32)
    B, Lq, Dm = x.shape
    xs = x.reshape(NCORES, NT1, Dm)
    r1 = _run(build_k1(), [{"x": _c(xs[c]), "g": _c(np.asarray(ab_norm_g[0], f32)), "w": _c(np.asarray(ab_w_in[0], f32))}
                           for c in range(NCORES)])
    FA = [np.concatenate([r1[b * 4 + s]["FA"] for s in range(4)], axis=1) for b in range(B)]
    FB = [np.concatenate([r1[b * 4 + s]["FB"] for s in range(4)], axis=1) for b in range(B)]
    NA = [np.concatenate([r1[b * 4 + s]["NA"] for s in range(4)], axis=0) for b in range(B)]
    NB = [np.concatenate([r1[b * 4 + s]["NB"] for s in range(4)], axis=0) for b in range(B)]
    in2 = []
    for c in range(NCORES):
        b, j = c // 4, c % 4
        in2.append({
            "gqT": _c(FA[b][j * 128:(j + 1) * 128]), "gkT": _c(FA[b][512 + j * 128:512 + (j + 1) * 128]),
            "glrT": _c(FA[b][1024:1040]), "gkN": _c(NA[b][:, j * 128:(j + 1) * 128]),
            "ggate": _c(NA[b][:, 512 + j * 256:512 + (j + 1) * 256]),
            "fgate": _c(NA[b][:, 1536 + j * 256:1536 + (j + 1) * 256]),
            "ff": _c(NA[b][:, 2560 + 2 * j:2560 + 2 * j + 2].reshape(NBLK, 128, 2).transpose(2, 1, 0)),
            "gv": _c(NB[b][:, j * 256:(j + 1) * 256]), "fv": _c(NB[b][:, 1024 + j * 256:1024 + (j + 1) * 256]),
            "fqT": _c(FB[b][j * 256:(j + 1) * 256]), "fkT": _c(FB[b][1024 + j * 256:1024 + (j + 1) * 256]),
            "aup": _c(np.asarray(gla_alpha_up[0], f32)[:, j * 128:(j + 1) * 128]),
            "ab": _c(np.asarray(gla_alpha_b[0], f32)[j * 128:(j + 1) * 128]),
            "hg": _c(np.asarray(gla_head_g[0], f32)[j * 256:(j + 1) * 256]),
            "fb": _c(np.asarray(fox_f_b[0], f32)[2 * j:2 * j + 2]),
        })
    r2 = _run(build_k2(), in2)
    mix = np.empty((B, Lq, Dm), f32)
    for c in range(NCORES):
        b, j = c // 4, c % 4
        mix[b][:, j * 256:(j + 1) * 256] = r2[c]["mix"][:, 0:256]
        mix[b][:, 1024 + j * 256:1024 + (j + 1) * 256] = r2[c]["mix"][:, 256:512]
    mixs = mix.reshape(NCORES, NT1, Dm)
    r3 = _run(build_k3(), [{"mix": _c(mixs[c]), "x": _c(xs[c]), "wo": _c(np.asarray(ab_w_out[0], f32)),
                            "g": _c(np.asarray(c_norm_g[0], f32)), "w": _c(np.asarray(c_w_in[0], f32))}
                           for c in range(NCORES)])
    u = np.empty((B, Lq, Dm), NPBF)
    for c in range(NCORES):
        b, s = c // 4, c % 4
        u[b][s * NT1:(s + 1) * NT1, :] = r3[c]["uT"].T
    in4 = []
    for c in range(NCORES):
        uc = u[:, :, c * 256:(c + 1) * 256].reshape(2, NCH, TC, 2, 128)
        uc = _c(uc.transpose(3, 4, 2, 0, 1).reshape(2, 128, TC, NCOL))
        g0 = c * 16
        lre = np.asarray(s5_lambda_re[0], f32)[g0:g0 + 16]
        lim = np.asarray(s5_lambda_im[0], f32)[g0:g0 + 16]
        lst = np.repeat(np.asarray(s5_log_step[0], f32)[g0:g0 + 16][:, None], 64, axis=1)
        in4.append({
            "u": uc, "lamre": _pairP(lre), "lamim": _pairP(lim), "lstep": _pairP(lst),
            "bre": _pairB(np.asarray(s5_b_re[0], f32)[g0:g0 + 16]), "bim": _pairB(np.asarray(s5_b_im[0], f32)[g0:g0 + 16]),
            "cre": _pairB(np.asarray(s5_c_re[0], f32)[g0:g0 + 16].transpose(0, 2, 1)),
            "cim": _pairB(np.asarray(s5_c_im[0], f32)[g0:g0 + 16].transpose(0, 2, 1)),
            "dT": _c(np.asarray(s5_d[0], f32)[c * 256:(c + 1) * 256].reshape(2, 128).T),
        })
    r4 = _run(build_k4(), in4)
    y = np.empty((B, Lq, Dm), f32)
    for c in range(NCORES):
        yc = r4[c]["y"].reshape(2, 128, 2, NCH, TC).transpose(2, 3, 4, 0, 1).reshape(2, Lq, 256)
        y[:, :, c * 256:(c + 1) * 256] = yc
    ys = y.reshape(NCORES, NT1, Dm)
    r5 = _run(build_k5(), [{"yT": _c(ys[c].T), "gT": _c(r3[c]["gT"]), "x1": _c(r3[c]["x1"]),
                            "wg": _c(np.asarray(glu_w[0], f32)), "bg": _c(np.asarray(glu_b[0], f32)),
                            "wo": _c(np.asarray(c_w_out[0], f32)), "gf": _c(np.asarray(final_norm_g, f32))}
                           for c in range(NCORES)])
    out = np.stack([r5[c]["out"] for c in range(NCORES)], axis=0).reshape(B, Lq, Dm)
    return out.astype(f32)
```
